# Optimizing a Trainium2 kernel written in Bass

```python
import jax, jax.numpy as jnp
from jax import lax
import numpy as np

D_MODEL = 1024
BATCH = 4
SEQ = 4096
DEPTH = 2
DEC_BATCH = 128
DEC_SEQ = 1
PAST_LEN = 16384
PAGE_SIZE = 128

HEAD_DIM = 64
MIX_WIDTH = 384
N_GROUPS = MIX_WIDTH // HEAD_DIM
N_BRANCH = 4
CONV_WIDTH = 3
SWA_WINDOW = 128
SWA_Q_HEADS = 6
SWA_KV_HEADS = 2
SWA_REP = SWA_Q_HEADS // SWA_KV_HEADS
CHUNK = 128
DIL_GROUPS = ((128, 1), (512, 4), (2048, 16))
DIL_HEADS = N_GROUPS // len(DIL_GROUPS)
BLOCK = 128
ROPE_THETA = 10000.0
EPS = 1e-6
NEG_INF = -1e30

SPLIT_SIZES = (MIX_WIDTH, MIX_WIDTH, MIX_WIDTH, MIX_WIDTH,
               SWA_Q_HEADS * HEAD_DIM, SWA_KV_HEADS * HEAD_DIM, SWA_KV_HEADS * HEAD_DIM, MIX_WIDTH,
               MIX_WIDTH, MIX_WIDTH, MIX_WIDTH,
               MIX_WIDTH, MIX_WIDTH, MIX_WIDTH, MIX_WIDTH)
IN_COLS = sum(SPLIT_SIZES)
SPLIT_POINTS = tuple(int(s) for s in np.cumsum(SPLIT_SIZES)[:-1])

kernel_name = "hybrid_parallel_gated_mixers_step"


def rms_norm(x, g):
    x32 = x.astype(jnp.float32)
    y = x32 * lax.rsqrt(jnp.mean(x32 * x32, axis=-1, keepdims=True) + EPS)
    return (y * g.astype(jnp.float32)).astype(x.dtype)


def layer_norm(x, g, b):
    x32 = x.astype(jnp.float32)
    mu = jnp.mean(x32, axis=-1, keepdims=True)
    var = jnp.mean(jnp.square(x32 - mu), axis=-1, keepdims=True)
    y = (x32 - mu) * lax.rsqrt(var + EPS)
    return (y * g.astype(jnp.float32) + b.astype(jnp.float32)).astype(x.dtype)


def silu(x):
    return x * jax.nn.sigmoid(x)


def rope(x, pos):
    half = x.shape[-1] // 2
    inv_freq = ROPE_THETA ** (-jnp.arange(half, dtype=jnp.float32) / half)
    ang = pos.astype(jnp.float32)[:, None] * inv_freq[None, :]
    cos = jnp.cos(ang)[:, None, :]
    sin = jnp.sin(ang)[:, None, :]
    x32 = x.astype(jnp.float32)
    x1, x2 = x32[..., :half], x32[..., half:]
    return jnp.concatenate([x1 * cos - x2 * sin, x2 * cos + x1 * sin], axis=-1).astype(x.dtype)


def masked_softmax(scores, mask, sink):
    s = jnp.where(mask, scores, jnp.float32(NEG_INF))
    m = jnp.max(s, axis=-1, keepdims=True)
    if sink is not None:
        m = jnp.maximum(m, sink)
    p = jnp.exp(s - m)
    den = jnp.sum(p, axis=-1, keepdims=True)
    if sink is not None:
        den = den + jnp.exp(sink - m)
    return p / den, (m + jnp.log(den))[..., 0]


def banded_window_attention(q, k, v, window, sink):
    Bn, L, Hkv, R, Dh = q.shape
    pad = (-L) % BLOCK
    Lp = L + pad
    nb = Lp // BLOCK

    def padseq(a):
        return jnp.pad(a, [(0, 0), (0, pad)] + [(0, 0)] * (a.ndim - 2))

    qb = padseq(q).reshape(Bn, nb, BLOCK, Hkv, R, Dh)
    kb = padseq(k).reshape(Bn, nb, BLOCK, Hkv, Dh)
    vb = padseq(v).reshape(Bn, nb, BLOCK, Hkv, Dh)
    kk = jnp.concatenate([jnp.concatenate([jnp.zeros_like(kb[:, :1]), kb[:, :-1]], axis=1), kb], axis=2)
    vv = jnp.concatenate([jnp.concatenate([jnp.zeros_like(vb[:, :1]), vb[:, :-1]], axis=1), vb], axis=2)
    scores = jnp.einsum('bnihrd,bnjhd->bnhrij', qb, kk, preferred_element_type=jnp.float32) * (Dh ** -0.5)
    i = jnp.arange(BLOCK)[:, None]
    j = jnp.arange(2 * BLOCK)[None, :]
    dist = i - j + BLOCK
    key_idx = jnp.arange(nb)[:, None, None] * BLOCK + j[None] - BLOCK
    mask = (dist >= 0)[None] & (dist <= window)[None] & (key_idx >= 0)
    mask = mask[None, :, None, None]
    sink_b = None if sink is None else sink.astype(jnp.float32)[None, None, :, :, None, None]
    p, lse = masked_softmax(scores, mask, sink_b)
    out = jnp.einsum('bnhrij,bnjhd->bnihrd', p.astype(v.dtype), vv).reshape(Bn, Lp, Hkv, R, Dh)[:, :L]
    lse = jnp.moveaxis(lse, -1, 2).reshape(Bn, Lp, Hkv, R)[:, :L]
    return out, lse


def gathered_window_attention(q, k_all, v_all, n_past, window, dilation, sink):
    S, Dh = q.shape[1], q.shape[-1]
    n_keys = window // dilation + 1
    idx = n_past + jnp.arange(S)[:, None] - dilation * jnp.arange(n_keys)[None, :]
    valid = idx >= 0
    idx = jnp.maximum(idx, 0)
    kg = k_all[:, idx]
    vg = v_all[:, idx]
    scores = jnp.einsum('bihrd,bikhd->bhrik', q, kg, preferred_element_type=jnp.float32) * (Dh ** -0.5)
    sink_b = None if sink is None else sink.astype(jnp.float32)[None, :, :, None, None]
    p, lse = masked_softmax(scores, valid[None, None, None], sink_b)
    out = jnp.einsum('bhrik,bikhd->bihrd', p.astype(v_all.dtype), vg)
    return out, jnp.moveaxis(lse, -1, 1)


def dilated_prompt_attention(q, k, v, window, dilation):
    Bn, S, H, Dh = q.shape
    L = S // dilation

    def split(a):
        return a.reshape(Bn, L, dilation, H, Dh).transpose(0, 2, 1, 3, 4).reshape(Bn * dilation, L, H, Dh)

    out, lse = banded_window_attention(split(q)[:, :, :, None], split(k), split(v), window // dilation, None)
    out = out[:, :, :, 0].reshape(Bn, dilation, L, H, Dh).transpose(0, 2, 1, 3, 4).reshape(Bn, S, H, Dh)
    lse = lse[..., 0].reshape(Bn, dilation, L, H).transpose(0, 2, 1, 3).reshape(Bn, S, H)
    return out, lse


def chunk_spatial_mix(v, w_s, b_s):
    Bn, S, _ = v.shape
    pad = (-S) % CHUNK
    vp = jnp.pad(v, ((0, 0), (0, pad), (0, 0))).reshape(Bn, (S + pad) // CHUNK, CHUNK, N_GROUPS, HEAD_DIM)
    w_causal = jnp.where(jnp.tril(jnp.ones((CHUNK, CHUNK), dtype=bool))[None], w_s, 0).astype(v.dtype)
    mixed = jnp.einsum('gts,bcsgd->bctgd', w_causal, vp) + b_s.T.astype(v.dtype)[None, None, :, :, None]
    return mixed.reshape(Bn, S + pad, MIX_WIDTH)[:, :S]


def mixer_layer(x, pos, past, norm_g, w_in, conv_w, sinks, v_ln_g, v_ln_b, w_spatial, b_spatial,
                w_branch, w_merge, w_out):
    Bn, S, _ = x.shape
    h = rms_norm(x, norm_g)
    (a_b, a_c, a_h, a_gate, s_q, s_k, s_v, s_gate,
     c_u, c_v, c_gate, d_q, d_k, d_v, d_gate) = jnp.split(h @ w_in, SPLIT_POINTS, axis=-1)

    z = a_c * a_h
    prev = jnp.zeros((Bn, CONV_WIDTH - 1, MIX_WIDTH), z.dtype) if past is None else past[0]
    zp = jnp.concatenate([prev, z], axis=1)
    conv = sum(conv_w[t] * zp[:, t:t + S] for t in range(CONV_WIDTH))
    y_a = a_b * conv
    new_conv = zp[:, -(CONV_WIDTH - 1):]

    q = rope(s_q.reshape(Bn, S, SWA_Q_HEADS, HEAD_DIM), pos).reshape(Bn, S, SWA_KV_HEADS, SWA_REP, HEAD_DIM)
    k = rope(s_k.reshape(Bn, S, SWA_KV_HEADS, HEAD_DIM), pos)
    v = s_v.reshape(Bn, S, SWA_KV_HEADS, HEAD_DIM)
    sink = sinks.reshape(SWA_KV_HEADS, SWA_REP)
    if past is None:
        o_b, _ = banded_window_attention(q, k, v, SWA_WINDOW, sink)
        k_all, v_all, keep = k, v, min(SWA_WINDOW, S)
    else:
        n_past = past[1].shape[1]
        k_all = jnp.concatenate([past[1][:, :, 0], k], axis=1)
        v_all = jnp.concatenate([past[1][:, :, 1], v], axis=1)
        o_b, _ = gathered_window_attention(q, k_all, v_all, n_past, SWA_WINDOW, 1, sink)
        keep = n_past
    y_b = o_b.reshape(Bn, S, MIX_WIDTH)
    new_swa = jnp.stack([k_all[:, -keep:], v_all[:, -keep:]], axis=2)

    vn = layer_norm(c_v, v_ln_g, v_ln_b)
    y_c = c_u * chunk_spatial_mix(vn, w_spatial, b_spatial)

    dq = rope(d_q.reshape(Bn, S, N_GROUPS, HEAD_DIM), pos)
    dk = rope(d_k.reshape(Bn, S, N_GROUPS, HEAD_DIM), pos)
    dv = d_v.reshape(Bn, S, N_GROUPS, HEAD_DIM)
    outs, lses, new_dil = [], [], []
    for g, (win, dil) in enumerate(DIL_GROUPS):
        qg = dq[:, :, g * DIL_HEADS:(g + 1) * DIL_HEADS]
        kg = dk[:, :, g * DIL_HEADS:(g + 1) * DIL_HEADS]
        vg = dv[:, :, g * DIL_HEADS:(g + 1) * DIL_HEADS]
        if past is None:
            o, lse = dilated_prompt_attention(qg, kg, vg, win, dil)
            k_all, v_all, keep = kg, vg, min(win, S)
        else:
            buf = past[2][g]
            n_past = buf.shape[1]
            k_all = jnp.concatenate([buf[:, :, 0], kg], axis=1)
            v_all = jnp.concatenate([buf[:, :, 1], vg], axis=1)
            o, lse = gathered_window_attention(qg[:, :, :, None], k_all, v_all, n_past, win, dil, None)
            o, lse = o[:, :, :, 0], lse[..., 0]
            keep = n_past
        outs.append(o)
        lses.append(lse)
        new_dil.append(jnp.stack([k_all[:, -keep:], v_all[:, -keep:]], axis=2))
    alpha = jax.nn.softmax(jnp.stack(lses, axis=0), axis=0)
    y_d = jnp.concatenate([o * alpha[g][..., None].astype(o.dtype) for g, o in enumerate(outs)],
                          axis=2).reshape(Bn, S, MIX_WIDTH)

    branches = jnp.stack([y_a * silu(a_gate), y_b * silu(s_gate), y_c * silu(c_gate), y_d * silu(d_gate)], axis=2)
    proj_b = jnp.einsum('bsne,ned->bsnd', branches, w_branch)
    gates = jax.nn.sigmoid(h @ w_merge).reshape(Bn, S, N_BRANCH, D_MODEL)
    merged = jnp.sum(gates * proj_b, axis=2)
    return x + merged @ w_out, new_conv, new_swa, new_dil, vn


def setup_inputs(seed: int = 0) -> dict:
    key = jax.random.key(seed)
    ks = jax.random.split(key, 19)

    def nrm(k, shape, scale):
        return scale * jax.random.normal(k, shape, jnp.float32)

    n_swa = min(SWA_WINDOW, PAST_LEN)
    dil_rows = [min(w, PAST_LEN) for w, _ in DIL_GROUPS]
    return {
        "x_prompt": nrm(ks[0], (BATCH, SEQ, D_MODEL), 1.0),
        "x_sample": nrm(ks[1], (DEC_BATCH, DEC_SEQ, D_MODEL), 1.0),
        "state_conv": nrm(ks[2], (DEPTH, DEC_BATCH, CONV_WIDTH - 1, MIX_WIDTH), 0.5),
        "cache_swa_kv": nrm(ks[3], (DEPTH, DEC_BATCH, n_swa, 2, SWA_KV_HEADS, HEAD_DIM), 1.0),
        "cache_dil1_kv": nrm(ks[4], (DEPTH, DEC_BATCH, dil_rows[0], 2, DIL_HEADS, HEAD_DIM), 1.0),
        "cache_dil4_kv": nrm(ks[5], (DEPTH, DEC_BATCH, dil_rows[1], 2, DIL_HEADS, HEAD_DIM), 1.0),
        "cache_dil16_kv": nrm(ks[6], (DEPTH, DEC_BATCH, dil_rows[2], 2, DIL_HEADS, HEAD_DIM), 1.0),
        "norm_g": 1.0 + nrm(ks[7], (DEPTH, D_MODEL), 0.05),
        "w_in": nrm(ks[8], (DEPTH, D_MODEL, IN_COLS), D_MODEL ** -0.5),
        "conv_w": nrm(ks[9], (DEPTH, CONV_WIDTH, MIX_WIDTH), CONV_WIDTH ** -0.5),
        "attn_sinks": nrm(ks[10], (DEPTH, SWA_Q_HEADS), 1.0),
        "v_ln_g": 1.0 + nrm(ks[11], (DEPTH, MIX_WIDTH), 0.05),
        "v_ln_b": nrm(ks[12], (DEPTH, MIX_WIDTH), 0.02),
        "w_spatial": nrm(ks[13], (DEPTH, N_GROUPS, CHUNK, CHUNK), 0.5 * CHUNK ** -0.5),
        "b_spatial": 1.0 + nrm(ks[14], (DEPTH, N_GROUPS, CHUNK), 0.1),
        "w_branch": nrm(ks[15], (DEPTH, N_BRANCH, MIX_WIDTH, D_MODEL), MIX_WIDTH ** -0.5),
        "w_merge": nrm(ks[16], (DEPTH, D_MODEL, N_BRANCH * D_MODEL), D_MODEL ** -0.5),
        "w_out": nrm(ks[17], (DEPTH, D_MODEL, D_MODEL), D_MODEL ** -0.5),
        "final_norm_g": 1.0 + nrm(ks[18], (D_MODEL,), 0.05),
    }


def reference(x_prompt, x_sample, state_conv, cache_swa_kv, cache_dil1_kv, cache_dil4_kv, cache_dil16_kv,
              norm_g, w_in, conv_w, attn_sinks, v_ln_g, v_ln_b, w_spatial, b_spatial,
              w_branch, w_merge, w_out, final_norm_g):
    pos_prompt = jnp.arange(SEQ, dtype=jnp.int32)
    pos_sample = PAST_LEN + jnp.arange(DEC_SEQ, dtype=jnp.int32)
    xp, xs = x_prompt, x_sample
    conv_p, conv_s, swa_p, swa_s, chunk_v_s = [], [], [], [], []
    dil_p = [[] for _ in DIL_GROUPS]
    dil_s = [[] for _ in DIL_GROUPS]
    for l in range(DEPTH):
        lw = (norm_g[l], w_in[l], conv_w[l], attn_sinks[l], v_ln_g[l], v_ln_b[l],
              w_spatial[l], b_spatial[l], w_branch[l], w_merge[l], w_out[l])
        xp, c_p, s_p, d_p, _ = mixer_layer(xp, pos_prompt, None, *lw)
        past = (state_conv[l], cache_swa_kv[l], (cache_dil1_kv[l], cache_dil4_kv[l], cache_dil16_kv[l]))
        xs, c_s, s_s, d_s, v_s = mixer_layer(xs, pos_sample, past, *lw)
        conv_p.append(c_p)
        conv_s.append(c_s)
        swa_p.append(s_p)
        swa_s.append(s_s)
        chunk_v_s.append(v_s)
        for g in range(len(DIL_GROUPS)):
            dil_p[g].append(d_p[g])
            dil_s[g].append(d_s[g])
    y_prompt = rms_norm(xp, final_norm_g)
    y_sample = rms_norm(xs, final_norm_g)
    return (y_prompt, y_sample,
            jnp.stack(conv_p), jnp.stack(conv_s),
            jnp.stack(swa_p), jnp.stack(swa_s),
            jnp.stack(dil_p[0]), jnp.stack(dil_s[0]),
            jnp.stack(dil_p[1]), jnp.stack(dil_s[1]),
            jnp.stack(dil_p[2]), jnp.stack(dil_s[2]),
            jnp.stack(chunk_v_s))
```

```python
import numpy as np
from contextlib import ExitStack
import concourse.bass as bass
import concourse.mybir as mybir
from concourse.bass_utils import run_bass_kernel_spmd

F32 = mybir.dt.float32
BF16 = mybir.dt.bfloat16
ALU = mybir.AluOpType
AF = mybir.ActivationFunctionType
AX = mybir.AxisListType

D = 1024
SEQ = 4096
TT = 512
NT = SEQ // TT
NS = 16
NL = 2
PAST = 16384
EPS = 1e-6
NCORES = 8
INC = 5248
SAME_SYNC = True
DBG = {"prep": True, "copy": True, "tiles": None, "consts": True}

GROUPS = [("d0", 128, 1), ("d1", 512, 4), ("d2", 2048, 16)]

WBLOCKS = [
    ("A0", 8, [("in", 384, 512)], ["ac0", "ac1", "ac2", "ah0"]),
    ("A1", 8, [("in", 896, 256), ("in", 0, 256)], ["ah1", "ah2", "ab0", "ab1"]),
    ("A2", 8, [("in", 256, 128), ("in", 1152, 384)], ["ab2", "ag0", "ag1", "ag2"]),
    ("C0", 8, [("in", 2560, 384), ("in", 3328, 128)], ["cu0", "cu1", "cu2", "cg0"]),
    ("C1", 8, [("in", 3456, 256)], ["cg1", "cg2"]),
    ("CV", 8, [("in", 2944, 384)], ["cv"]),
    ("B0", 8, [("in", 1920, 64), ("in", 1920, 64), ("in", 1984, 64), ("in", 1984, 64),
               ("in", 2048, 64), ("in", 2048, 64), ("in", 2112, 64), ("in", 2112, 64)],
     ["ka", "kb", "va", "vb"]),
    ("B1", 8, [("in", 1536, 384), ("in", 2176, 128)], ["sq0", "sq1", "sq2", "sg0"]),
    ("B2", 8, [("in", 2304, 256), ("in", 4096, 256)], ["sg1", "sg2", "dk0", "dk1"]),
    ("D0", 8, [("in", 4352, 512)], ["dk2", "dv0", "dv1", "dv2"]),
    ("D1", 8, [("in", 3712, 384), ("in", 4864, 128)], ["dq0", "dq1", "dq2", "dg0"]),
    ("D2", 8, [("in", 4992, 256)], ["dg1", "dg2"]),
]
for _n in range(4):
    for _hf in range(2):
        WBLOCKS.append((f"M{_n}{_hf}", 8, [("merge", _n * 1024 + _hf * 512, 512)], ["m"]))
        WBLOCKS.append((f"R{_n}{_hf}", 3, [("branch", _n, _hf * 512, 512)], ["r"]))
for _hf in range(2):
    WBLOCKS.append((f"O{_hf}", 8, [("out", _hf * 512, 512)], ["o"]))
NBLK = len(WBLOCKS)


def _mask_list():
    ids = {}
    tabs = []
    for gi, (_, win, dil) in enumerate(GROUPS):
        for dlt in range(-3, win // 128 + 1):
            interior = (128 * dlt - 127 >= 0) and (128 * dlt + 511 <= win)
            key = (gi, "int") if interior else (gi, dlt)
            if key not in ids:
                ids[key] = len(tabs)
                tabs.append((gi, dlt))
            ids[(gi, dlt)] = ids[key]
    return ids, tabs


MASK_IDS, MASK_TABS = _mask_list()
NMASK = len(MASK_TABS)


def _host_consts():
    c = {}
    c["c_ident"] = np.eye(128, dtype=np.float32)
    perm = np.zeros((128, 128), np.float32)
    for m in range(128):
        if m % 64 < 32:
            perm[m + 32, m] = -1.0
        else:
            perm[m - 32, m] = 1.0
    c["c_perm"] = perm
    bo = np.zeros((128, 128), np.float32)
    bo[:64, :64] = 1.0
    bo[64:, 64:] = 1.0
    c["c_bones"] = bo
    half = 32
    inv_freq = (np.float32(10000.0) ** (-np.arange(half, dtype=np.float32) / np.float32(half))).astype(np.float32)
    pos = np.concatenate([np.arange(SEQ, dtype=np.float32), np.full(NS, PAST, np.float32)])
    ang = (pos[None, :] * inv_freq[np.arange(128) % 32][:, None]).astype(np.float32)
    c["c_cos"] = np.cos(ang).astype(np.float32)
    c["c_sin"] = np.sin(ang).astype(np.float32)
    masks = np.zeros((NMASK, 128, 512), np.float32)
    ki = np.arange(128)[:, None]
    qi = np.arange(512)[None, :]
    for mid, (gi, dlt) in enumerate(MASK_TABS):
        _, win, dil = GROUPS[gi]
        diff = 128 * dlt + qi - ki
        masks[mid] = ((diff >= 0) & (diff <= win) & (diff % dil == 0)).astype(np.float32)
    c["c_mask"] = masks
    c["c_tril"] = (np.arange(128)[:, None] <= np.arange(128)[None, :]).astype(np.float32)
    sel = np.zeros((2, 128), np.float32)
    sel[0, :64] = 1.0
    sel[1, 64:] = 1.0
    c["c_sel"] = sel
    return c


class Eng:
    def __init__(self, key):
        self.key = key
        self.cnt = 0
        self.seen = {}
        self.prog = []
        self.log = []


class Buf:
    def __init__(self, t, name, st=None):
        self.t = t
        self.name = name
        self.st = st if st is not None else {"w": None, "r": {}, "dsem": None}

    w = property(lambda s: s.st["w"], lambda s, v: s.st.__setitem__("w", v))
    r = property(lambda s: s.st["r"], lambda s, v: s.st.__setitem__("r", v))
    dsem = property(lambda s: s.st["dsem"], lambda s, v: s.st.__setitem__("dsem", v))

    def view(self, ap):
        return Buf(ap, self.name, self.st)

    def __getitem__(self, k):
        return self.t[k]


class KB:
    def __init__(self, nc, es):
        self.nc = nc
        self.es = es
        self.E = {k: Eng(k) for k in ("pe", "act", "dve", "pool", "sp")}
        self.sems = {}
        self.dcnt = {}
        for k in ("pe", "act", "dve", "pool"):
            self.sems[k] = es.enter_context(nc.semaphore("s_" + k))
        self.nd = 0

    def sb(self, name, shape, dt):
        return Buf(self.es.enter_context(self.nc.sbuf_tensor(name, list(shape), dt)), name)

    def ps(self, name, shape, dt):
        b = Buf(self.es.enter_context(self.nc.psum_tensor(name, list(shape), dt)), name)
        b.st["excl"] = True
        return b

    def dsem(self, b):
        if b.dsem is None:
            self.nd += 1
            k = f"d{self.nd}_{b.name}"
            self.sems[k] = self.es.enter_context(self.nc.semaphore(k[:24]))
            self.dcnt[k] = 0
            b.dsem = k
        return b.dsem

    def _deps(self, reads, writes):
        d = {}
        for b in reads:
            if b.w:
                d[b.w[0]] = max(d.get(b.w[0], 0), b.w[1])
            if b.st.get("excl"):
                for k, v in b.r.items():
                    d[k] = max(d.get(k, 0), v)
        for b in writes:
            if b.w:
                d[b.w[0]] = max(d.get(b.w[0], 0), b.w[1])
            for k, v in b.r.items():
                d[k] = max(d.get(k, 0), v)
        return d

    def _wait(self, E, d):
        for k, v in d.items():
            if k == E.key and (k == "pe" or not DBG.get("same", SAME_SYNC)):
                continue
            if k in self.dcnt:
                v = self.dcnt[k]
            if E.seen.get(k, 0) >= v:
                continue
            E.seen[k] = v
            sem = self.sems[k]
            E.log.append(("wait", k, v))
            E.prog.append(lambda h, sem=sem, v=v: h.wait_ge(sem, v))

    def op(self, ek, fn, reads=(), writes=(), inc=True):
        E = self.E[ek]
        self._wait(E, self._deps(reads, writes))
        if inc:
            E.cnt += 1
            tick = E.cnt
            sem = self.sems[ek]
            E.log.append(("inc", ek, 1))
            E.prog.append(lambda h, fn=fn, sem=sem: fn(h).then_inc(sem, 1))
        else:
            tick = E.cnt + 1
            E.prog.append(lambda h, fn=fn: fn(h))
        for b in reads:
            b.r[ek] = max(b.r.get(ek, 0), tick)
        for b in writes:
            b.w = (ek, tick)
            b.r = {}

    def dma(self, qk, out, in_, reads=(), writes=(), semb=None, slow=False):
        Q = self.E[qk]
        sk = self.dsem(semb)
        d = self._deps(reads, writes)
        d.pop(sk, None)
        self._wait(Q, d)
        self.dcnt[sk] += 16
        tick = self.dcnt[sk]
        sem = self.sems[sk]
        Q.log.append(("inc", sk, 16))
        if slow:
            Q.prog.append(lambda h, o=out, i=in_, sem=sem: h.dma_start(
                out=o, in_=i, allow_slow_non_contiguous=True).then_inc(sem, 16))
        else:
            Q.prog.append(lambda h, o=out, i=in_, sem=sem: h.dma_start(out=o, in_=i).then_inc(sem, 16))
        for b in reads:
            b.r[sk] = tick
        for b in writes:
            b.w = (sk, tick)
            b.r = {}

    def final_wait(self, qk):
        Q = self.E[qk]
        for k, v in self.dcnt.items():
            if v > 0 and Q.seen.get(k, 0) < v:
                sem = self.sems[k]
                Q.prog.append(lambda h, sem=sem, v=v: h.wait_ge(sem, v))
        for k in ("pe", "act", "dve", "pool"):
            v = self.E[k].cnt
            if v > 0:
                sem = self.sems[k]
                Q.prog.append(lambda h, sem=sem, v=v: h.wait_ge(sem, v))

    def simulate(self):
        val = {k: 0 for k in self.sems}
        pc = {k: 0 for k in self.E}
        progress = True
        while progress:
            progress = False
            for k, e in self.E.items():
                while pc[k] < len(e.log):
                    kind, sk, v = e.log[pc[k]]
                    if kind == "wait":
                        if val[sk] < v:
                            break
                    else:
                        val[sk] += v
                    pc[k] += 1
                    progress = True
        stuck = {k: (pc[k], len(e.log), e.log[pc[k]], val[e.log[pc[k]][1]]) for k, e in self.E.items() if pc[k] < len(e.log)}
        return stuck

    def emit(self):
        nc = self.nc
        with nc.Block() as block:
            @block.tensor
            def _(h):
                for f in self.E["pe"].prog:
                    f(h)

            @block.scalar
            def _(h):
                for f in self.E["act"].prog:
                    f(h)

            @block.vector
            def _(h):
                for f in self.E["dve"].prog:
                    f(h)

            @block.gpsimd
            def _(h):
                for f in self.E["pool"].prog:
                    f(h)

            @block.sync
            def _(h):
                for f in self.E["sp"].prog:
                    f(h)


def build_program():
    nc = bass.Bass("TRN2", target_bir_lowering=False)
    es = ExitStack()
    K = KB(nc, es)

    def din(name, shape):
        return nc.dram_tensor(name, list(shape), F32, kind="ExternalInput").ap()

    def dout(name, shape):
        return nc.dram_tensor(name, list(shape), F32, kind="ExternalOutput").ap()

    x_in = din("x", [SEQ, D])
    xs_in = din("xs", [NS, D])
    stc_in = din("state_conv", [NL, NS, 2, 384])
    cache_in = {"sw": din("cache_swa", [NL, NS, 128, 256]), "d0": din("cache_d0", [NL, NS, 128, 256]),
                "d1": din("cache_d1", [NL, NS, 512, 256]), "d2": din("cache_d2", [NL, NS, 2048, 256])}
    norm_g = din("norm_g", [NL, D])
    w_in = din("w_in", [NL, D, INC])
    conv_w = din("conv_w", [NL, 3, 384])
    sinks = din("attn_sinks", [NL, 6])
    vln_g = din("v_ln_g", [NL, 384])
    vln_b = din("v_ln_b", [NL, 384])
    w_sp = din("w_spatial", [NL, 6, 128, 128])
    b_sp = din("b_spatial", [NL, 6, 128])
    w_br = din("w_branch", [NL, 4, 384, D])
    w_mg = din("w_merge", [NL, D, 4 * D])
    w_out = din("w_out", [NL, D, D])
    fin_g = din("final_norm_g", [1, D])
    c_ident = din("c_ident", [128, 128])
    c_perm = din("c_perm", [128, 128])
    c_bones = din("c_bones", [128, 128])
    c_cos = din("c_cos", [128, SEQ + NS])
    c_sin = din("c_sin", [128, SEQ + NS])
    c_mask = din("c_mask", [NMASK, 128, 512])
    c_tril = din("c_tril", [128, 128])
    c_sel = din("c_sel", [2, 128])

    y_p = dout("y_p", [SEQ, D])
    y_s = dout("y_s", [NS, D])
    o_conv_p = dout("o_conv_p", [NL, 2, 384])
    o_conv_s = dout("o_conv_s", [NL, NS, 2, 384])
    o_kv_p = {"sw": dout("o_sw_p", [NL, 128, 256]), "d0": dout("o_d0_p", [NL, 128, 256]),
              "d1": dout("o_d1_p", [NL, 512, 256]), "d2": dout("o_d2_p", [NL, 2048, 256])}
    o_kv_s = {"sw": dout("o_sw_s", [NL, NS, 128, 256]), "d0": dout("o_d0_s", [NL, NS, 128, 256]),
              "d1": dout("o_d1_s", [NL, NS, 512, 256]), "d2": dout("o_d2_s", [NL, NS, 2048, 256])}
    o_cv_s = dout("o_cv_s", [NL, NS, 384])

    wsc_t = nc.dram_tensor("wsc", [NL * NBLK, 128, 8, 512], BF16).ap()
    x1_t = nc.dram_tensor("x1sc", [SEQ + NS, D], F32).ap()
    wsc = Buf(wsc_t, "wsc")
    x1b = Buf(x1_t, "x1sc")
    dummy_out = Buf(None, "outs")

    sb, ps = K.sb, K.ps
    identf = sb("identf", [128, 128], F32)
    identb = sb("identb", [128, 128], BF16)
    permb = sb("permb", [128, 128], BF16)
    bonesb = sb("bonesb", [128, 128], BF16)
    maskb = sb("maskb", [128, NMASK, 512], BF16)
    trilf = sb("trilf", [128, 128], F32)
    self_ = sb("self", [2, 128], F32)
    cst = Buf(None, "cst")
    for dst, src in ((identf, c_ident), (trilf, c_tril)):
        K.dma("sp", dst[:], src[:, :], writes=[dst], semb=cst)
    K.dma("sp", self_[:], c_sel[:, :], writes=[self_], semb=cst)
    cstp = Buf(None, "cstp")
    for dst, src in ((identb, c_ident), (permb, c_perm), (bonesb, c_bones)):
        K.dma("pool", dst[:], src[:, :], writes=[dst], semb=cstp)
    for m in range(NMASK):
        K.dma("pool", maskb[:, m, :], c_mask[m, :, :], writes=[maskb], semb=cstp)

    gbc = sb("gbc", [128, D], F32)
    convw = sb("convw", [128, NL, 3, 3], F32)
    lng = sb("lng", [128, 2, 384], F32)
    esink = sb("esink", [128, NL, 6], F32)
    wct = sb("wct", [128, NL, 6, 128], BF16)
    bsp = sb("bsp", [2, NL, 3, 128], F32)
    biasC = sb("biasC", [128, NL, 3, 128], F32)
    w00 = sb("w00", [128, NL, 6], F32)
    b00 = sb("b00", [128, NL, 3], F32)
    for l in range(NL):
        K.dma("sp", esink[:, l, :], sinks[l:l + 1, :].partition_broadcast(128), writes=[esink], semb=cst)
        for j in range(3):
            K.dma("sp", convw[:, l, j, :], conv_w[l][:, j * 128:(j + 1) * 128].rearrange("t p -> p t"),
                  writes=[convw], semb=cst, slow=True)
        K.dma("sp", bsp[:, l, :, :], b_sp[l].rearrange("(j two) t -> two j t", two=2), writes=[bsp], semb=cst)
        K.dma("sp", w00[:, l, :], w_sp[l, :, 0:1, 0:1].rearrange("g a b -> (a b) g").partition_broadcast(128),
              writes=[w00], semb=cst, slow=True)
        for j in range(3):
            for hh in range(2):
                K.dma("sp", b00[hh * 64:(hh + 1) * 64, l, j:j + 1],
                      b_sp[l, 2 * j + hh:2 * j + hh + 1, 0:1].partition_broadcast(64),
                      writes=[b00], semb=cst)
    gfin = sb("gfin", [128, D], F32)
    K.dma("sp", gfin[:], fin_g[0:1, :].partition_broadcast(128), writes=[gfin], semb=cst)
    K.op("act", lambda h: h.activation(out=esink[:], in_=esink[:], func=AF.Exp), reads=[esink], writes=[esink])
    epsT = sb("epsT", [128, 1], F32)
    K.op("pool", lambda h: h.memset(epsT[:], EPS), writes=[epsT])

    PJ = [ps(f"pj{i}", [128, 512], F32) for i in range(2)]
    ST = [ps(f"st{i}", [128, 512], F32) for i in range(2)]
    PV = [ps(f"pv{i}", [128, 512], F32) for i in range(2)]
    TRB = ps("trb", [128, 1024], BF16)
    TRF = ps("trf", [128, 512], F32)
    rot = {"pj": 0, "st": 0, "pv": 0, "w": 0, "pm": 0, "t": 0}

    def nxt(lst, key):
        rot[key] += 1
        return lst[rot[key] % len(lst)]

    CVT = sb("CVT", [128, 384], F32)
    wcf = CVT.view(CVT.t[:, 0:128])
    ROW = CVT.view(CVT.t[:, 128:384].rearrange("p (o c) -> p o c", o=1))
    for l in range(NL):
        for g in range(6):
            K.dma("sp", wcf[:], w_sp[l, g, :, :], writes=[wcf], semb=wcf)
            K.op("pe", lambda h: h.transpose(out=TRF[:, 0:128], in_=wcf[:], identity=identf[:]),
                 reads=[wcf, identf], writes=[TRF])
            K.op("dve", lambda h, l=l, g=g: h.tensor_tensor(out=wct[:, l, g, :], in0=TRF[:, 0:128], in1=trilf[:],
                                                             op=ALU.mult), reads=[TRF, trilf], writes=[wct])
        for j in range(3):
            K.op("pe", lambda h, l=l, j=j: h.matmul(TRF[:, 0:128], lhsT=self_[:, :], rhs=bsp[:, l, j, :],
                                                     start=True, stop=True), reads=[self_, bsp], writes=[TRF])
            K.op("act", lambda h, l=l, j=j: h.copy(out=biasC[:, l, j, :], in_=TRF[:, 0:128]),
                 reads=[TRF], writes=[biasC])

    XT = sb("XT", [128, 4, D], F32)
    MG = sb("MG", [128, 8, 512], BF16)
    hT = sb("hT", [128, 8, 512], BF16)
    stg = [XT.view(XT.t[:, :, :].rearrange("p b (h c) -> p (b h) c", h=2))]
    stgb = [hT, MG]
    cast_engs = ["pool", "dve", "act"]
    ci = 0
    for l in range(NL if DBG["prep"] else 0):
        for bi, (bname, nkc, segs, roles) in enumerate(WBLOCKS):
            s32, s16 = stg[0], stgb[ci % 2]
            c0 = 0
            for seg in segs:
                if seg[0] == "in":
                    src = w_in[l].rearrange("(kc p) c -> p kc c", p=128)[:, :, seg[1]:seg[1] + seg[2]]
                    n = seg[2]
                elif seg[0] == "merge":
                    src = w_mg[l].rearrange("(kc p) c -> p kc c", p=128)[:, :, seg[1]:seg[1] + seg[2]]
                    n = seg[2]
                elif seg[0] == "out":
                    src = w_out[l].rearrange("(kc p) c -> p kc c", p=128)[:, :, seg[1]:seg[1] + seg[2]]
                    n = seg[2]
                else:
                    src = w_br[l, seg[1]].rearrange("(kc p) c -> p kc c", p=128)[:, :, seg[2]:seg[2] + seg[3]]
                    n = seg[3]
                K.dma("sp", s32[:, 0:nkc, c0:c0 + n], src, writes=[s32], semb=s32)
                c0 += n
            ek = cast_engs[ci % 3]
            if ek == "act":
                K.op(ek, lambda h, a=s16, b=s32, nkc=nkc, c0=c0: h.copy(out=a[:, 0:nkc, 0:c0], in_=b[:, 0:nkc, 0:c0]),
                     reads=[s32], writes=[s16])
            else:
                K.op(ek, lambda h, a=s16, b=s32, nkc=nkc, c0=c0: h.tensor_copy(out=a[:, 0:nkc, 0:c0],
                                                                               in_=b[:, 0:nkc, 0:c0]),
                     reads=[s32], writes=[s16])
            K.dma("sp", wsc_t[l * NBLK + bi, :, 0:nkc, 0:c0], s16[:, 0:nkc, 0:c0], reads=[s16], writes=[wsc], semb=s16)
            ci += 1

    cpy = Buf(None, "cpy")
    for l in range(NL if DBG["copy"] else 0):
        for nm, n in (("sw", 128), ("d0", 128), ("d1", 512), ("d2", 2048)):
            for s0 in range(0, NS, 4):
                K.dma("sp", o_kv_s[nm][l, s0:s0 + 4, 0:n - 1, :], cache_in[nm][l, s0:s0 + 4, 1:n, :],
                      writes=[dummy_out], semb=cpy)

    NW = 3
    wring = [sb(f"wr{i}", [128, 8, 512], BF16) for i in range(NW)]
    junk = sb("junk", [128, D], BF16)
    junkf = junk.view(junk.t[:, :].bitcast(F32))
    hb = sb("hb", [128, D], BF16)
    ss = sb("ss", [128, 8], F32)
    CS = sb("CS", [128, 2, 512], F32)
    T32 = [sb(f"t32_{i}", [128, 3, 512], F32) for i in range(3)]
    TMP3 = sb("TMP3", [128, 1, NS], F32)
    Z = sb("Z", [128, 3, 2 + 512], F32)
    BR = [sb(f"br{i}", [128, 3, 512], BF16) for i in range(4)]
    _pad = sb("padq", [128, 256], BF16)
    QT = sb("QT", [128, 3, 512], BF16)
    QB = [sb(f"qb{i}", [128, 512], BF16) for i in range(2)]
    PMx = [sb(f"pmx{i}", [128, 512], BF16) for i in range(2)]
    PMm = [sb(f"pmm{i}", [128, 512], BF16) for i in range(2)]
    SG = [sb(f"sg{i}", [128, 512], BF16) for i in range(2)]
    RD = sb("RD", [128, 512], F32)
    TMPD = sb("TMPD", [128, 512], F32)
    VTt = sb("VTt", [128, 512], BF16)
    F32K = T32[0]
    F32V = T32[1]
    VNZ = sb("VNZ", [128, 6, 128], BF16)
    LNS = sb("LNS", [128, 8], F32)
    KTs = {"sa": sb("KTsa", [128, 1024], BF16), "sb": sb("KTsb", [128, 1024], BF16),
           "d0": sb("KTd0", [128, 1024], BF16), "d1": sb("KTd1", [128, 1024], BF16),
           "d2": sb("KTd2", [128, 4096], BF16)}
    KTR = {"sa": 1024, "sb": 1024, "d0": 1024, "d1": 1024, "d2": 4096}
    VTK = {"sw": sb("VTKsw", [128, 5, 4, 128], BF16), "d0": sb("VTKd0", [128, 5, 2, 128], BF16),
           "d1": sb("VTKd1", [128, 8, 2, 128], BF16), "d2": sb("VTKd2", [128, 20, 2, 128], BF16)}
    VTR = {"sw": 5, "d0": 5, "d1": 8, "d2": 20}
    for nm in VTK:
        K.op("pool", lambda h, b=VTK[nm]: h.memset(b[:], 1.0), writes=[VTK[nm]])
    K.op("pool", lambda h: h.memset(VNZ[:], 0.0), writes=[VNZ])
    for _t in T32:
        K.op("pool", lambda h, _t=_t: h.memset(_t[:], 0.0), writes=[_t])
    CK = [sb(f"ck{i}", [128, 256], F32) for i in range(2)]
    KD = sb("KD", [128, 2, 128], BF16)
    KTS = sb("KTS", [128, 2, 128], BF16)
    VS = sb("VS", [128, 4, 128], BF16)
    K.op("pool", lambda h: h.memset(VS[:], 1.0), writes=[VS])
    PS_ = sb("PS_", [128, 12], BF16)
    KO = sb("KO", [128, 5, NS], BF16)
    VO = sb("VO", [128, 5, NS], F32)
    STC = XT.view(XT.t[0:NS, 1, 0:768])
    STT = sb("STT", [128, 6, NS], F32)
    PRD = sb("PRD", [128, NS], BF16)
    POWN = sb("POWN", [128, 6, NS], F32)
    PVS = sb("PVS", [128, 12 * NS], F32)
    STS = ps("sts", [128, 512], F32) if False else None

    wstate = {"seq": [], "next": 0, "loaded": {}}

    for l in range(NL):
        for t in range(NT + 1):
            if DBG["tiles"] is None or (l, t) in DBG["tiles"]:
                for bi in range(NBLK):
                    wstate["seq"].append((l, bi))

    def wload(i):
        l, bi = wstate["seq"][i]
        nkc = WBLOCKS[bi][1]
        ncol = sum(s[-1] for s in WBLOCKS[bi][2])
        slot = wring[i % NW]
        K.dma("sp", slot[:, 0:nkc, 0:ncol], wsc_t[l * NBLK + bi, :, 0:nkc, 0:ncol], reads=[wsc], writes=[slot], semb=slot)

    def wget():
        i = wstate["next"]
        wstate["next"] += 1
        while wstate.get("issued", 0) < min(i + 2, len(wstate["seq"])):
            wload(wstate.get("issued", 0))
            wstate["issued"] = wstate.get("issued", 0) + 1
        return wring[i % NW]

    class Lazy:
        def __init__(self):
            self.slots = []

        def __getitem__(self, k):
            while len(self.slots) <= k:
                self.slots.append(wget())
            return self.slots[k]

    def ev(ek, out, in_):
        if ek == "act":
            return lambda h: h.copy(out=out, in_=in_)
        return lambda h: h.tensor_copy(out=out, in_=in_)

    def tile(l, T):
        smp = (T == NT)
        nt = NS if smp else TT
        nb = 1 if smp else 4
        pb = NS if smp else 128
        tok0 = SEQ if smp else T * TT
        xsrc = (xs_in if smp else x_in) if l == 0 else x1_t
        if smp:
            src = xs_in[:, :] if l == 0 else x1_t[SEQ:SEQ + NS, :]
            K.dma("sp", XT[0:NS, 0, :], src, reads=([x1b] if l else []), writes=[XT], semb=XT)
        else:
            src = (x_in if l == 0 else x1_t)[tok0:tok0 + TT, :].rearrange("(b p) c -> p b c", p=128)
            K.dma("sp", XT[:, :, :], src, reads=([x1b] if l else []), writes=[XT], semb=XT)
        K.dma("sp", CS[:, 0, 0:nt], c_cos[:, tok0:tok0 + nt], writes=[CS], semb=CS)
        K.dma("sp", CS[:, 1, 0:nt], c_sin[:, tok0:tok0 + nt], writes=[CS], semb=CS)
        for b in range(nb):
            K.op("act", lambda h, b=b: h.activation(out=junk[0:pb, :], in_=XT[0:pb, b, :], func=AF.Square),
                 reads=[XT], writes=[junk])
            K.op("dve", lambda h, b=b: h.reduce_sum(out=ss[0:pb, b:b + 1], in_=junk[0:pb, :], axis=AX.X),
                 reads=[junk], writes=[ss])
        K.op("act", lambda h: h.activation(out=ss[0:pb, 0:nb], in_=ss[0:pb, 0:nb], func=AF.Sqrt, scale=1.0 / D,
                                            bias=epsT[0:pb, 0:1]), reads=[ss, epsT], writes=[ss])
        K.op("dve", lambda h: h.reciprocal(out=ss[0:pb, 4:4 + nb], in_=ss[0:pb, 0:nb]), reads=[ss], writes=[ss])
        for b in range(nb):
            K.op("dve", lambda h, b=b: h.scalar_tensor_tensor(out=hb[0:pb, :], in0=XT[0:pb, b, :],
                                                               scalar=ss[0:pb, 4 + b:5 + b], in1=gbc[0:pb, :],
                                                               op0=ALU.mult, op1=ALU.mult),
                 reads=[XT, ss, gbc], writes=[hb])
            for kc in range(8):
                K.op("pe", lambda h, kc=kc: h.transpose(out=TRB[:, kc * 128:kc * 128 + pb],
                                                        in_=hb[0:pb, kc * 128:(kc + 1) * 128],
                                                        identity=identb[0:pb, 0:pb]),
                     reads=[hb, identb], writes=[TRB], inc=(kc == 7))
            K.op("act", lambda h, b=b: h.copy(
                out=hT[:, :, b * 128:b * 128 + pb],
                in_=TRB[:, :].rearrange("p (k t) -> p k t", k=8)[:, :, 0:pb]), reads=[TRB], writes=[hT])

        def proj(wslot, ci_):
            pj = nxt(PJ, "pj")
            for kc in range(8):
                K.op("pe", lambda h, kc=kc, pj=pj: h.matmul(pj[:, 0:nt], lhsT=wslot[:, kc, ci_ * 128:(ci_ + 1) * 128],
                                                            rhs=hT[:, kc, 0:nt], start=(kc == 0), stop=(kc == 7)),
                     reads=[wslot, hT], writes=[pj], inc=(kc == 7))
            if DBG.get("dump") is not None:
                DBG["dumpn"] = DBG.get("dumpn", 0) + 1
                if DBG["dumpn"] in DBG["dump"]:
                    di = DBG["dump"].index(DBG["dumpn"])
                    K.op("act", ev("act", MG[:, 0:2, 0:nt].bitcast(F32)[:, 0, :] if False else junkf[:, 0:nt], pj[:, 0:nt]),
                         reads=[pj], writes=[junkf])
                    K.dma("sp", y_p[di * 128:(di + 1) * 128, 0:nt], junkf[:, 0:nt], reads=[junkf], writes=[dummy_out], semb=junkf)
            return pj

        def rope(pj, dst_bf, dst_f32=None):
            rc = DBG.get("ropecut", 9)
            qb = nxt(QB, "t")
            K.op("act", ev("act", qb[:, 0:nt], pj[:, 0:nt]), reads=[pj], writes=[qb])
            if rc <= 1:
                return
            p2 = nxt(ST, "st")
            K.op("pe", lambda h: h.matmul(p2[:, 0:nt], lhsT=permb[:, :], rhs=qb[:, 0:nt], start=True, stop=True),
                 reads=[permb, qb], writes=[p2])
            if rc <= 2:
                return
            K.op("dve", lambda h: h.tensor_tensor(out=TMPD[:, 0:nt], in0=pj[:, 0:nt], in1=CS[:, 0, 0:nt], op=ALU.mult),
                 reads=[pj, CS] + ([qb] if DBG.get("serial", True) else []), writes=[TMPD])
            if rc <= 3:
                return
            K.op("dve", lambda h: h.tensor_tensor(out=RD[:, 0:nt], in0=p2[:, 0:nt], in1=CS[:, 1, 0:nt], op=ALU.mult),
                 reads=[p2, CS], writes=[RD])
            if rc <= 4:
                return
            if dst_f32 is not None:
                K.op("dve", lambda h: h.scalar_tensor_tensor(out=dst_f32[0], in0=RD[:, 0:nt], scalar=1.0, in1=TMPD[:, 0:nt],
                                                              op0=ALU.mult, op1=ALU.add), reads=[TMPD, RD], writes=[dst_f32[1]])
                K.op("act", ev("act", dst_bf[0], dst_f32[0]), reads=[dst_f32[1]], writes=[dst_bf[1]])
            elif DBG.get("rope3", True):
                K.op("dve", lambda h: h.scalar_tensor_tensor(out=dst_bf[0], in0=RD[:, 0:nt], scalar=1.0, in1=TMPD[:, 0:nt],
                                                              op0=ALU.mult, op1=ALU.add), reads=[TMPD, RD], writes=[dst_bf[1]])
            elif DBG.get("rope2", True):
                K.op("pool", lambda h: h.tensor_tensor(out=TMPD[:, 0:nt], in0=TMPD[:, 0:nt], in1=RD[:, 0:nt], op=ALU.add),
                     reads=[TMPD, RD], writes=[TMPD])
                K.op("act", ev("act", dst_bf[0], TMPD[:, 0:nt]), reads=[TMPD], writes=[dst_bf[1]])
            else:
                K.op(DBG.get("ropeeng", "pool"), lambda h: h.tensor_tensor(out=dst_bf[0], in0=TMPD[:, 0:nt], in1=RD[:, 0:nt], op=ALU.add),
                     reads=[TMPD, RD], writes=[dst_bf[1]])

        def kcols(nm, pos, n):
            c = pos % KTR[nm]
            return slice(c, c + n)

        def vfill(nm, pj, layouts, want32):
            K.op("act", ev("act", VTt[:, 0:nt], pj[:, 0:nt]), reads=[pj], writes=[VTt])
            if want32 is not None:
                K.op("dve", ev("dve", want32[0], pj[:, 0:nt]), reads=[pj], writes=[want32[1]])
            if smp:
                return
            for b in range(4):
                K.op("pe", lambda h, b=b: h.transpose(out=TRB[:, 0:128], in_=VTt[:, b * 128:(b + 1) * 128],
                                                      identity=identb[:, :]), reads=[VTt, identb], writes=[TRB])
                slot = (4 * T + b) % VTR[nm]
                ve = DBG.get("vfilleng", "dve")
                for (lay, c0, s0) in layouts:
                    K.op(ve, ev(ve, VTK[nm][:, slot, lay, c0:c0 + 64], TRB[:, s0:s0 + 64]),
                         reads=[TRB], writes=[VTK[nm]])

        def attend(nm_k, nm_v, lay, gi, qchunk, ph):
            win = GROUPS[gi][1]
            kb_lo = max(0, 4 * T - win // 128)
            kbs = list(range(kb_lo, 4 * T + 4))
            pv = nxt(PV, "pv")
            r0 = ph * 64
            for i, kb in enumerate(kbs):
                st = nxt(ST, "st")
                K.op("pe", lambda h, st=st, kb=kb: h.matmul(st[:, :], lhsT=KTs[nm_k][r0:r0 + 64, kcols(nm_k, kb * 128, 128)],
                                                            rhs=QT[r0:r0 + 64, qchunk, :], start=True, stop=True),
                     reads=[KTs[nm_k], QT], writes=[st])
                px = nxt(PMx, "pm")
                K.op("act", lambda h, st=st, px=px: h.activation(out=px[:], in_=st[:], func=AF.Exp, scale=0.125),
                     reads=[st], writes=[px])
                pm = PMm[rot["pm"] % 2]
                mid = MASK_IDS[(gi, 4 * T - kb)]
                K.op("pool", lambda h, px=px, pm=pm, mid=mid: h.tensor_tensor(out=pm[:], in0=px[:], in1=maskb[:, mid, :],
                                                                               op=ALU.mult),
                     reads=[px, maskb], writes=[pm])
                slot = kb % VTR[nm_v]
                K.op("pe", lambda h, pv=pv, pm=pm, slot=slot, i=i: h.matmul(
                    pv[:, :], lhsT=VTK[nm_v][:, slot, lay, :], rhs=pm[:, :], start=(i == 0), stop=(i == len(kbs) - 1)),
                    reads=[VTK[nm_v], pm], writes=[pv], inc=(i == len(kbs) - 1))
            return pv

        def kv_rows(nm, kbuf, vbuf, kh, vh, b_list, row0):
            for b in b_list:
                chunks = [(kbuf, 0), (kbuf, 1), (vbuf, 0), (vbuf, 1)] if nm == "sw" else [(kbuf, 0), (vbuf, 0)]
                for q, (src, ci_) in enumerate(chunks):
                    K.op("pe", lambda h, q=q, src=src, ci_=ci_, b=b: h.transpose(
                        out=TRF[:, q * 128:(q + 1) * 128], in_=src[:, ci_, b * 128:(b + 1) * 128], identity=identf[:, :]),
                        reads=[src, identf], writes=[TRF], inc=(q == len(chunks) - 1))
                if nm == "sw":
                    for q in range(4):
                        K.op("act", ev("act", ROW[:, 0, q * 64:(q + 1) * 64], TRF[:, q * 128:q * 128 + 64]),
                             reads=[TRF], writes=[ROW])
                else:
                    K.op("act", ev("act", ROW[:, 0, :], TRF[:, 0:256]), reads=[TRF], writes=[ROW])
                r = row0 + b * 128
                K.dma("sp", o_kv_p[nm][l, r:r + 128, :], ROW[:, 0, :], reads=[ROW], writes=[dummy_out], semb=ROW)

        if DBG.get("stage", 99) <= 0:
            return
        wA = Lazy()
        AC, CV_, YA = T32[0], T32[1], T32[2]

        def chunk(role):
            order = ["ac0", "ac1", "ac2", "ah0", "ah1", "ah2", "ab0", "ab1", "ab2", "ag0", "ag1", "ag2"]
            i = order.index(role)
            return wA[i // 4], i % 4

        if not smp and T == 0:
            K.op("pool", lambda h: h.memset(Z[:, :, 0:2], 0.0), writes=[Z])
        elif not smp:
            K.op("act", ev("act", Z[:, :, 0:2], Z[:, :, TT:TT + 2]), reads=[Z], writes=[Z])
        if smp:
            K.dma("sp", STC[:, :], stc_in[l].rearrange("s t f -> s (t f)"), writes=[STC], semb=STC)
            for q in range(6):
                K.op("pe", lambda h, q=q: h.transpose(out=TRF[:, q * NS:(q + 1) * NS], in_=STC[:, q * 128:(q + 1) * 128],
                                                      identity=identf[0:NS, 0:NS]), reads=[STC, identf], writes=[TRF],
                     inc=(q == 5))
            K.op("act", ev("act", STT[:, :, :], TRF[:, 0:6 * NS].rearrange("p (q s) -> p q s", q=6)),
                 reads=[TRF], writes=[STT])
        for j in range(3):
            pj = proj(*chunk(f"ac{j}"))
            K.op("act", ev("act", AC[:, j, 0:nt], pj[:, 0:nt]), reads=[pj], writes=[AC])
        for j in range(3):
            pj = proj(*chunk(f"ah{j}"))
            K.op("dve", lambda h, j=j, pj=pj: h.tensor_tensor(out=Z[:, j, 2:2 + nt], in0=pj[:, 0:nt], in1=AC[:, j, 0:nt],
                                                              op=ALU.mult), reads=[pj, AC], writes=[Z])
            if smp:
                taps = [STT[:, j, :], STT[:, 3 + j, :], Z[:, j, 2:2 + nt]]
            else:
                taps = [Z[:, j, 0:nt], Z[:, j, 1:1 + nt], Z[:, j, 2:2 + nt]]
            K.op("dve", lambda h, j=j, taps=taps: h.tensor_scalar(out=CV_[:, j, 0:nt], in0=taps[0],
                                                                  scalar1=convw[:, l, j, 0:1], scalar2=None, op0=ALU.mult),
                 reads=[Z, STT, convw], writes=[CV_])
            for tp in (1, 2):
                K.op("dve", lambda h, j=j, taps=taps, tp=tp: h.scalar_tensor_tensor(
                    out=CV_[:, j, 0:nt], in0=taps[tp], scalar=convw[:, l, j, tp:tp + 1], in1=CV_[:, j, 0:nt],
                    op0=ALU.mult, op1=ALU.add), reads=[Z, STT, convw, CV_], writes=[CV_])
        if smp:
            for j in range(3):
                K.dma("sp", o_conv_s[l, :, 0, j * 128:(j + 1) * 128].rearrange("s p -> p s"), STT[:, 3 + j, :],
                      reads=[STT], writes=[dummy_out], semb=STT, slow=True)
                K.dma("sp", o_conv_s[l, :, 1, j * 128:(j + 1) * 128].rearrange("s p -> p s"), Z[:, j, 2:2 + NS],
                      reads=[Z], writes=[dummy_out], semb=Z, slow=True)
        elif T == NT - 1:
            for j in range(3):
                K.dma("sp", o_conv_p[l][:, j * 128:(j + 1) * 128].rearrange("t p -> p t"), Z[:, j, TT:TT + 2],
                      reads=[Z], writes=[dummy_out], semb=Z, slow=True)
        for j in range(3):
            pj = proj(*chunk(f"ab{j}"))
            K.op("dve", lambda h, j=j, pj=pj: h.tensor_tensor(out=YA[:, j, 0:nt], in0=pj[:, 0:nt], in1=CV_[:, j, 0:nt],
                                                              op=ALU.mult), reads=[pj, CV_], writes=[YA])
        for j in range(3):
            pj = proj(*chunk(f"ag{j}"))
            sg = nxt(SG, "t")
            K.op("act", lambda h, pj=pj, sg=sg: h.activation(out=sg[:, 0:nt], in_=pj[:, 0:nt], func=AF.Silu),
                 reads=[pj], writes=[sg])
            K.op("pool", lambda h, j=j, sg=sg: h.tensor_tensor(out=BR[0][:, j, 0:nt], in0=YA[:, j, 0:nt], in1=sg[:, 0:nt],
                                                               op=ALU.mult), reads=[YA, sg], writes=[BR[0]])

        if DBG.get("stage", 99) <= 1:
            return
        wC = Lazy()
        CU, SGC = T32[0], T32[1]
        for j in range(3):
            pj = proj(wC[0], j)
            K.op("act", ev("act", CU[:, j, 0:nt], pj[:, 0:nt]), reads=[pj], writes=[CU])
        for j in range(3):
            pj = proj(wC[0], 3) if j == 0 else proj(wC[1], j - 1)
            K.op("act", lambda h, j=j, pj=pj: h.activation(out=SGC[:, j, 0:nt], in_=pj[:, 0:nt], func=AF.Silu),
                 reads=[pj], writes=[SGC])
        mixps = [nxt(PV, "pv"), nxt(ST, "st"), nxt(PV, "pv")]
        for b in range(nb):
            pj = nxt(PJ, "pj")
            for kc in range(8):
                K.op("pe", lambda h, kc=kc, pj=pj, b=b: h.matmul(pj[0:pb, 0:384], lhsT=hT[:, kc, b * 128:b * 128 + pb],
                                                                 rhs=wC[2][:, kc, 0:384], start=(kc == 0), stop=(kc == 7)),
                     reads=[wC[2], hT], writes=[pj], inc=(kc == 7))
            K.op("act", ev("act", CVT[0:pb, :], pj[0:pb, 0:384]), reads=[pj], writes=[CVT])
            K.op("dve", lambda h: h.reduce_sum(out=LNS[0:pb, 0:1], in_=CVT[0:pb, :], axis=AX.X), reads=[CVT], writes=[LNS])
            K.op("dve", lambda h: h.tensor_scalar(out=LNS[0:pb, 1:2], in0=LNS[0:pb, 0:1], scalar1=-1.0 / 384, scalar2=None,
                                                   op0=ALU.mult), reads=[LNS], writes=[LNS])
            K.op("dve", lambda h: h.tensor_scalar(out=CVT[0:pb, :], in0=CVT[0:pb, :], scalar1=LNS[0:pb, 1:2], scalar2=None,
                                                   op0=ALU.add), reads=[CVT, LNS], writes=[CVT])
            K.op("act", lambda h: h.activation(out=junk[0:pb, 0:384], in_=CVT[0:pb, :], func=AF.Square),
                 reads=[CVT], writes=[junk])
            K.op("dve", lambda h: h.reduce_sum(out=LNS[0:pb, 2:3], in_=junk[0:pb, 0:384], axis=AX.X),
                 reads=[junk], writes=[LNS])
            K.op("act", lambda h: h.activation(out=LNS[0:pb, 3:4], in_=LNS[0:pb, 2:3], func=AF.Sqrt, scale=1.0 / 384,
                                                bias=epsT[0:pb, 0:1]), reads=[LNS, epsT], writes=[LNS])
            K.op("dve", lambda h: h.reciprocal(out=LNS[0:pb, 4:5], in_=LNS[0:pb, 3:4]), reads=[LNS], writes=[LNS])
            K.op("dve", lambda h: h.scalar_tensor_tensor(out=CVT[0:pb, :], in0=CVT[0:pb, :], scalar=LNS[0:pb, 4:5],
                                                          in1=lng[0:pb, 0, :], op0=ALU.mult, op1=ALU.mult),
                 reads=[CVT, LNS, lng], writes=[CVT])
            K.op("pool", lambda h: h.tensor_tensor(out=CVT[0:pb, :], in0=CVT[0:pb, :], in1=lng[0:pb, 1, :], op=ALU.add),
                 reads=[CVT, lng], writes=[CVT])
            if smp:
                K.dma("sp", o_cv_s[l, :, :], CVT[0:NS, :], reads=[CVT], writes=[dummy_out], semb=CVT)
            if smp:
                K.op("dve", lambda h: h.tensor_tensor(
                    out=CVT[0:pb, :].rearrange("p (g d) -> p g d", g=6),
                    in0=CVT[0:pb, :].rearrange("p (g d) -> p g d", g=6),
                    in1=w00[0:pb, l, :].unsqueeze(2).to_broadcast([pb, 6, 64]), op=ALU.mult),
                    reads=[CVT, w00], writes=[CVT])
            cv3 = CVT[0:pb, :].rearrange("p (j two d) -> p j two d", j=3, two=2)
            vz = VNZ[0:pb, :, :].rearrange("p (j two) (h d) -> p j two h d", two=2, h=2)
            K.op("act", ev("act", vz[:, :, 0, 0, :], cv3[:, :, 0, :]), reads=[CVT], writes=[VNZ])
            K.op("act", ev("act", vz[:, :, 1, 1, :], cv3[:, :, 1, :]), reads=[CVT], writes=[VNZ])
            for j in range(3):
                mp = mixps[j]
                for gg in range(2):
                    g = 2 * j + gg
                    rhs = identb[0:pb, 0:pb] if smp else wct[:, l, g, :]
                    K.op("pe", lambda h, mp=mp, g=g, gg=gg, rhs=rhs, b=b: h.matmul(
                        mp[:, b * 128:b * 128 + pb], lhsT=VNZ[0:pb, g, :], rhs=rhs, start=(gg == 0), stop=(gg == 1)),
                        reads=[VNZ, wct, identb], writes=[mp], inc=(gg == 1))
        for j in range(3):
            mp = mixps[j]
            if smp:
                K.op("dve", lambda h, j=j, mp=mp: h.scalar_tensor_tensor(
                    out=TMPD[:, 0:nt], in0=mp[:, 0:nt], scalar=b00[:, l, j:j + 1], in1=CU[:, j, 0:nt],
                    op0=ALU.add, op1=ALU.mult), reads=[mp, b00, CU], writes=[TMPD])
            else:
                K.op("dve", lambda h, j=j, mp=mp: h.tensor_tensor(
                    out=TMPD[:, :].rearrange("p (b t) -> p b t", b=4), in0=mp[:, :].rearrange("p (b t) -> p b t", b=4),
                    in1=biasC[:, l, j, :].unsqueeze(1).to_broadcast([128, 4, 128]), op=ALU.add),
                    reads=[mp, biasC], writes=[TMPD])
                K.op("pool", lambda h, j=j: h.tensor_tensor(out=TMPD[:, 0:nt], in0=TMPD[:, 0:nt], in1=CU[:, j, 0:nt],
                                                            op=ALU.mult), reads=[TMPD, CU], writes=[TMPD])
            K.op("pool", lambda h, j=j: h.tensor_tensor(out=BR[2][:, j, 0:nt], in0=TMPD[:, 0:nt], in1=SGC[:, j, 0:nt],
                                                        op=ALU.mult), reads=[TMPD, SGC], writes=[BR[2]])

        if DBG.get("stage", 99) <= 2:
            return
        wB = Lazy()
        last = (not smp) and (T == NT - 1)
        want = last or smp
        for q, nm in enumerate(("sa", "sb")):
            pj = proj(wB[0], q)
            dstb = (KO[:, q, :], KO) if smp else (KTs[nm][:, kcols(nm, tok0, TT)], KTs[nm])
            if DBG.get("nokrope"):
                K.op("act", ev("act", dstb[0], pj[:, 0:nt]), reads=[pj], writes=[dstb[1]])
            else:
                rope(pj, dstb, (F32K[:, q, 0:nt], F32K) if want else None)
        for q in range(2):
            pj = proj(wB[0], 2 + q)
            if smp:
                K.op("act", ev("act", VO[:, q, :], pj[:, 0:nt]), reads=[pj], writes=[VO])
            else:
                if DBG.get("sub", 99) <= 0:
                    return
                vfill("sw", pj, [(2 * q, 0, 0), (2 * q + 1, 64, 0)], (F32V[:, q, 0:nt], F32V) if want else None)
        if DBG.get("sub", 99) <= 1:
            return
        if smp:
            sample_newrow(l, "sw", None)
        if last:
            kv_rows("sw", F32K, F32V, [(0, 0), (1, 0)], [(0, 0), (1, 0)], [3], 128 - 512)
        for j in range(DBG.get("nq", 3)):
            if DBG.get("fetchonly"):
                wB[1]
                continue
            pj = proj(wB[1], j)
            if DBG.get("nocopy"):
                continue
            if DBG.get("norope"):
                qd = BR[1] if DBG.get("qdst") == "br" else QT
                K.op("act", ev("act", qd[:, j, 0:nt], pj[:, 0:nt]), reads=[pj], writes=[qd])
            else:
                qd = BR[1] if DBG.get("qdst") == "br" else QT
                rope(pj, (qd[:, j, 0:nt], qd))
        YB = T32[2]
        if DBG.get("sub", 99) <= 2:
            return
        if not smp:
            for hd in range(6):
                j, ph = hd // 2, hd % 2
                nm = "sa" if hd < 3 else "sb"
                lay = (0 if hd < 3 else 2) + ph
                pv = attend(nm, "sw", lay, 0, j, ph)
                r0, o0 = ph * 64, (1 - ph) * 64
                K.op("dve", lambda h, pv=pv, r0=r0, o0=o0, hd=hd: h.tensor_scalar(
                    out=RD[r0:r0 + 64, :], in0=pv[o0:o0 + 64, :], scalar1=esink[o0:o0 + 64, l, hd:hd + 1], scalar2=None,
                    op0=ALU.add), reads=[pv, esink], writes=[RD])
                K.op("dve", lambda h, r0=r0: h.reciprocal(out=RD[r0:r0 + 64, :], in_=RD[r0:r0 + 64, :]),
                     reads=[RD], writes=[RD])
                K.op("dve", lambda h, pv=pv, r0=r0, j=j: h.tensor_tensor(out=YB[r0:r0 + 64, j, :], in0=pv[r0:r0 + 64, :],
                                                                          in1=RD[r0:r0 + 64, :], op=ALU.mult),
                     reads=[pv, RD], writes=[YB])
        else:
            sample_attention(l, YB, TMP3)
        for j in range(3):
            pj = proj(wB[1], 3) if j == 0 else proj(wB[2], j - 1)
            sg = nxt(SG, "t")
            K.op("act", lambda h, pj=pj, sg=sg: h.activation(out=sg[:, 0:nt], in_=pj[:, 0:nt], func=AF.Silu),
                 reads=[pj], writes=[sg])
            K.op("pool", lambda h, j=j, sg=sg: h.tensor_tensor(out=BR[1][:, j, 0:nt], in0=YB[:, j, 0:nt], in1=sg[:, 0:nt],
                                                               op=ALU.mult), reads=[YB, sg], writes=[BR[1]])

        if DBG.get("stage", 99) <= 3:
            return
        wD = Lazy()
        keepT = {"d0": [NT - 1], "d1": [NT - 1], "d2": [4, 5, 6, 7]}
        for g in range(3):
            nm = GROUPS[g][0]
            pj = proj(wB[2], 2 + g) if g < 2 else proj(wD[0], 0)
            w32 = smp or (T in keepT[nm])
            dstb = (KO[:, 2 + g, :], KO) if smp else (KTs[nm][:, kcols(nm, tok0, TT)], KTs[nm])
            rope(pj, dstb, (F32K[:, 0, 0:nt], F32K) if w32 else None)
            pj = proj(wD[0], 1 + g)
            if smp:
                K.op("act", ev("act", VO[:, 2 + g, :], pj[:, 0:nt]), reads=[pj], writes=[VO])
            else:
                vfill(nm, pj, [(0, 0, 0), (1, 64, 64)], (F32V[:, 0, 0:nt], F32V) if w32 else None)
                if w32:
                    blist = [3] if nm == "d0" else [0, 1, 2, 3]
                    row0 = {"d0": 128 - 512, "d1": 0, "d2": (T - 4) * 512}[nm]
                    kv_rows(nm, F32K, F32V, [(0, 0), (0, 64)], [(0, 0), (0, 64)], blist, row0)
            if smp:
                sample_newrow(l, nm, g)
        for j in range(3):
            pj = proj(wD[1], j)
            rope(pj, (QT[:, j, 0:nt], QT))
        NUM, DEN, YD = T32[0], T32[1], T32[2]
        if not smp:
            for g in range(3):
                nm = GROUPS[g][0]
                for ph in range(2):
                    pv = attend(nm, nm, ph, g, g, ph)
                    r0, o0 = ph * 64, (1 - ph) * 64
                    K.op("act", ev("act", NUM[r0:r0 + 64, g, :], pv[r0:r0 + 64, :]), reads=[pv], writes=[NUM])
                    K.op("act", ev("act", DEN[r0:r0 + 64, g, :], pv[o0:o0 + 64, :]), reads=[pv], writes=[DEN])
        else:
            sample_attention_d(l, NUM, DEN)
        K.op("pool", lambda h: h.tensor_tensor(out=DEN[:, 0, 0:nt], in0=DEN[:, 0, 0:nt], in1=DEN[:, 1, 0:nt], op=ALU.add),
             reads=[DEN], writes=[DEN])
        K.op("pool", lambda h: h.tensor_tensor(out=DEN[:, 0, 0:nt], in0=DEN[:, 0, 0:nt], in1=DEN[:, 2, 0:nt], op=ALU.add),
             reads=[DEN], writes=[DEN])
        K.op("dve", lambda h: h.reciprocal(out=RD[:, 0:nt], in_=DEN[:, 0, 0:nt]), reads=[DEN], writes=[RD])
        for g in range(3):
            K.op("dve", lambda h, g=g: h.tensor_tensor(out=YD[:, g, 0:nt], in0=NUM[:, g, 0:nt], in1=RD[:, 0:nt], op=ALU.mult),
                 reads=[NUM, RD], writes=[YD])
        for j in range(3):
            pj = proj(wD[1], 3) if j == 0 else proj(wD[2], j - 1)
            sg = nxt(SG, "t")
            K.op("act", lambda h, pj=pj, sg=sg: h.activation(out=sg[:, 0:nt], in_=pj[:, 0:nt], func=AF.Silu),
                 reads=[pj], writes=[sg])
            K.op("pool", lambda h, j=j, sg=sg: h.tensor_tensor(out=BR[3][:, j, 0:nt], in0=YD[:, j, 0:nt], in1=sg[:, 0:nt],
                                                               op=ALU.mult), reads=[YD, sg], writes=[BR[3]])

        if DBG.get("stage", 99) <= 4:
            return
        for n in range(4):
            for hf in range(2):
                wM = wget()
                wR = wget()
                for c4 in range(4):
                    oc = hf * 4 + c4
                    pg = proj(wM, c4)
                    sg = nxt(SG, "t")
                    K.op("act", lambda h, pg=pg, sg=sg: h.activation(out=sg[:, 0:nt], in_=pg[:, 0:nt], func=AF.Sigmoid),
                         reads=[pg], writes=[sg])
                    pp = nxt(ST, "st")
                    for kc in range(3):
                        K.op("pe", lambda h, kc=kc, pp=pp, c4=c4, wR=wR, n=n: h.matmul(
                            pp[:, 0:nt], lhsT=wR[:, kc, c4 * 128:(c4 + 1) * 128], rhs=BR[n][:, kc, 0:nt],
                            start=(kc == 0), stop=(kc == 2)), reads=[wR, BR[n]], writes=[pp], inc=(kc == 2))
                    if n == 0:
                        K.op("dve", lambda h, pp=pp, sg=sg, oc=oc: h.tensor_tensor(out=MG[:, oc, 0:nt], in0=pp[:, 0:nt],
                                                                                   in1=sg[:, 0:nt], op=ALU.mult),
                             reads=[pp, sg], writes=[MG])
                    else:
                        K.op("dve", lambda h, pp=pp, sg=sg: h.tensor_tensor(out=TMPD[:, 0:nt], in0=pp[:, 0:nt],
                                                                            in1=sg[:, 0:nt], op=ALU.mult),
                             reads=[pp, sg], writes=[TMPD])
                        dst = MG
                        K.op("pool", lambda h, oc=oc, dst=dst: h.tensor_tensor(out=dst[:, oc, 0:nt], in0=MG[:, oc, 0:nt],
                                                                               in1=TMPD[:, 0:nt], op=ALU.add),
                             reads=[MG, TMPD], writes=[dst])
        if DBG.get("stage", 99) <= 5:
            return
        wO = Lazy()
        for b in range(nb):
            for hf in range(2):
                po = nxt(PJ, "pj")
                for kc in range(8):
                    K.op("pe", lambda h, kc=kc, po=po, b=b, hf=hf: h.matmul(
                        po[0:pb, :], lhsT=MG[:, kc, b * 128:b * 128 + pb], rhs=wO[hf][:, kc, :],
                        start=(kc == 0), stop=(kc == 7)), reads=[wO[hf], MG], writes=[po], inc=(kc == 7))
                K.op("dve", lambda h, po=po, b=b, hf=hf: h.tensor_tensor(
                    out=XT[0:pb, b, hf * 512:(hf + 1) * 512], in0=po[0:pb, :], in1=XT[0:pb, b, hf * 512:(hf + 1) * 512],
                    op=ALU.add), reads=[po, XT], writes=[XT])
        if l == 0:
            if DBG.get("x1out") and not smp:
                K.dma("sp", y_p[tok0:tok0 + TT, :].rearrange("(b p) c -> p b c", p=128), XT[:, :, :], reads=[XT],
                      writes=[dummy_out], semb=XT)
            if smp:
                K.dma("sp", x1_t[SEQ:SEQ + NS, :], XT[0:NS, 0, :], reads=[XT], writes=[x1b], semb=XT)
            else:
                K.dma("sp", x1_t[tok0:tok0 + TT, :].rearrange("(b p) c -> p b c", p=128), XT[:, :, :], reads=[XT],
                      writes=[x1b], semb=XT)
        else:
            for b in range(nb):
                K.op("act", lambda h, b=b: h.activation(out=junk[0:pb, :], in_=XT[0:pb, b, :], func=AF.Square),
                     reads=[XT], writes=[junk])
                K.op("dve", lambda h, b=b: h.reduce_sum(out=ss[0:pb, b:b + 1], in_=junk[0:pb, :], axis=AX.X),
                     reads=[junk], writes=[ss])
            K.op("act", lambda h: h.activation(out=ss[0:pb, 0:nb], in_=ss[0:pb, 0:nb], func=AF.Sqrt, scale=1.0 / D,
                                                bias=epsT[0:pb, 0:1]), reads=[ss, epsT], writes=[ss])
            K.op("dve", lambda h: h.reciprocal(out=ss[0:pb, 4:4 + nb], in_=ss[0:pb, 0:nb]), reads=[ss], writes=[ss])
            for b in range(nb):
                K.op("dve", lambda h, b=b: h.scalar_tensor_tensor(out=XT[0:pb, b, :], in0=XT[0:pb, b, :],
                                                                   scalar=ss[0:pb, 4 + b:5 + b], in1=gfin[0:pb, :],
                                                                   op0=ALU.mult, op1=ALU.mult),
                     reads=[XT, ss, gfin], writes=[XT])
            if smp:
                K.dma("sp", y_s[:, :], XT[0:NS, 0, :], reads=[XT], writes=[dummy_out], semb=XT)
            else:
                K.dma("sp", y_p[tok0:tok0 + TT, :].rearrange("(b p) c -> p b c", p=128), XT[:, :, :], reads=[XT],
                      writes=[dummy_out], semb=XT)

    def sample_newrow(l, nm, g):
        if DBG.get("nonewrow"):
            return
        n = {"sw": 128, "d0": 128, "d1": 512, "d2": 2048}[nm]
        if nm == "sw":
            srcs = [(F32K, 0, 0), (F32K, 1, 0), (F32V, 0, 0), (F32V, 1, 0)]
        else:
            srcs = [(F32K, 0, 0), (F32K, 0, 64), (F32V, 0, 0), (F32V, 0, 64)]
        if nm == "sw":
            for q in range(2):
                K.op("dve", ev("dve", F32V[:, q, 0:NS], VO[:, q, :]), reads=[VO], writes=[F32V])
        else:
            K.op("dve", ev("dve", F32V[:, 0, 0:NS], VO[:, 2 + g, :]), reads=[VO], writes=[F32V])
        chunks = [(F32K, 0), (F32K, 1), (F32V, 0), (F32V, 1)] if nm == "sw" else [(F32K, 0), (F32V, 0)]
        for q, (src, ci_) in enumerate(chunks):
            K.op("pe", lambda h, q=q, src=src, ci_=ci_: h.transpose(out=TRF[:, q * 128:(q + 1) * 128], in_=src[:, ci_, 0:128],
                                                                    identity=identf[:, :]),
                 reads=[src, identf], writes=[TRF], inc=(q == len(chunks) - 1))
        if nm == "sw":
            for q in range(4):
                K.op("act", ev("act", ROW[0:NS, 0, q * 64:(q + 1) * 64], TRF[0:NS, q * 128:q * 128 + 64]),
                     reads=[TRF], writes=[ROW])
        else:
            K.op("act", ev("act", ROW[0:NS, 0, :], TRF[0:NS, 0:256]), reads=[TRF], writes=[ROW])
        if not DBG.get("nonewdma"):
            K.dma("sp", o_kv_s[nm][l, :, n - 1, :], ROW[0:NS, 0, :], reads=[ROW], writes=[dummy_out], semb=ROW)

    def sample_cache_heads(l, nm, s, heads, STSp, PVSp):
        n, dil = {"sw": (128, 1), "d0": (128, 1), "d1": (512, 4), "d2": (2048, 16)}[nm]
        ck = CK[rot["t"] % 2]
        rot["t"] += 1
        K.dma("sp", ck[:, :], cache_in[nm][l, s, 0:n:dil, :], writes=[ck], semb=ck)
        if nm == "sw":
            for dup in range(2):
                K.op("act", ev("act", KD[:, :, dup * 64:(dup + 1) * 64],
                               ck[:, 0:128].rearrange("p (k d) -> p k d", k=2)), reads=[ck], writes=[KD])
            for q in range(2):
                K.op("pe", lambda h, q=q: h.transpose(out=TRB[:, q * 128:(q + 1) * 128], in_=KD[:, q, :], identity=identb[:, :]),
                     reads=[KD, identb], writes=[TRB], inc=(q == 1))
            K.op("dve", ev("dve", KTS[:, :, :], TRB[:, 0:256].rearrange("p (q t) -> p q t", q=2)), reads=[TRB], writes=[KTS])
            for q in range(2):
                K.op("pool", ev("pool", VS[:, 2 * q, 0:64], ck[:, 128 + q * 64:192 + q * 64]), reads=[ck], writes=[VS])
                K.op("pool", ev("pool", VS[:, 2 * q + 1, 64:128], ck[:, 128 + q * 64:192 + q * 64]), reads=[ck], writes=[VS])
        else:
            K.op("act", ev("act", KD[:, 0, :], ck[:, 0:128]), reads=[ck], writes=[KD])
            K.op("pe", lambda h: h.transpose(out=TRB[:, 0:128], in_=KD[:, 0, :], identity=identb[:, :]),
                 reads=[KD, identb], writes=[TRB])
            K.op("dve", ev("dve", KTS[:, 0, :], TRB[:, 0:128]), reads=[TRB], writes=[KTS])
            K.op("pool", ev("pool", VS[:, 0, 0:64], ck[:, 128:192]), reads=[ck], writes=[VS])
            K.op("pool", ev("pool", VS[:, 1, 64:128], ck[:, 192:256]), reads=[ck], writes=[VS])
        nh = len(heads)
        for i, (hidx, qc, ph, kt, vl) in enumerate(heads):
            r0 = ph * 64
            K.op("pe", lambda h, i=i, qc=qc, r0=r0, kt=kt: h.matmul(
                STSp[:, i:i + 1], lhsT=KTS[r0:r0 + 64, kt, :], rhs=QT[r0:r0 + 64, qc, s:s + 1], start=True, stop=True),
                reads=[KTS, QT], writes=[STSp], inc=(i == nh - 1))
        K.op("act", lambda h: h.activation(out=PS_[:, 0:nh], in_=STSp[:, 0:nh], func=AF.Exp, scale=0.125),
             reads=[STSp], writes=[PS_])
        for i, (hidx, qc, ph, kt, vl) in enumerate(heads):
            K.op("pe", lambda h, i=i, hidx=hidx, vl=vl: h.matmul(
                PVSp[:, hidx * NS + s:hidx * NS + s + 1], lhsT=VS[:, vl, :], rhs=PS_[:, i:i + 1], start=True, stop=True),
                reads=[VS, PS_], writes=[PVSp], inc=(i == nh - 1))

    def own_key(l, pairs, slot):
        for (r0, nr, qc, ko) in pairs:
            K.op("dve", lambda h, r0=r0, nr=nr, qc=qc, ko=ko: h.tensor_tensor(
                out=PRD[r0:r0 + nr, :], in0=QT[r0:r0 + nr, qc, 0:NS], in1=KO[r0:r0 + nr, ko, :], op=ALU.mult),
                reads=[QT, KO], writes=[PRD])
        K.op("pe", lambda h: h.matmul(TRF[:, 256:256 + NS], lhsT=bonesb[:, :], rhs=PRD[:, :], start=True, stop=True),
             reads=[bonesb, PRD], writes=[TRF])
        K.op("act", lambda h, slot=slot: h.activation(out=POWN[:, slot, :], in_=TRF[:, 256:256 + NS], func=AF.Exp, scale=0.125),
             reads=[TRF], writes=[POWN])

    def sample_attention(l, YB, TMP3):
        PVSp = PV[0]
        STSp = ST[0]
        for s in range(NS):
            heads = []
            for hd in range(6):
                heads.append((hd, hd // 2, hd % 2, 0 if hd < 3 else 1, (0 if hd < 3 else 2) + hd % 2))
            sample_cache_heads(l, "sw", s, heads, STSp, PVSp)
        K.op("act", ev("act", PVS[:, 0:6 * NS], PVSp[:, 0:6 * NS]), reads=[PVSp], writes=[PVS])
        own_key(l, [(0, 128, 0, 0)], 0)
        own_key(l, [(0, 64, 1, 0), (64, 64, 1, 1)], 1)
        own_key(l, [(0, 128, 2, 1)], 2)
        for hd in range(6):
            j, ph = hd // 2, hd % 2
            r0, o0 = ph * 64, (1 - ph) * 64
            vo = 0 if hd < 3 else 1
            cs = slice(hd * NS, (hd + 1) * NS)
            K.op("act", ev("act", TMPD[r0:r0 + 64, 0:NS], PVS[o0:o0 + 64, cs]), reads=[PVS], writes=[TMPD])
            K.op("dve", lambda h, r0=r0, j=j: h.tensor_tensor(out=TMPD[r0:r0 + 64, 0:NS], in0=TMPD[r0:r0 + 64, 0:NS],
                                                               in1=POWN[r0:r0 + 64, j, :], op=ALU.add),
                 reads=[TMPD, POWN], writes=[TMPD])
            K.op("dve", lambda h, r0=r0, hd=hd: h.tensor_scalar(out=RD[r0:r0 + 64, 0:NS], in0=TMPD[r0:r0 + 64, 0:NS],
                                                                scalar1=esink[r0:r0 + 64, l, hd:hd + 1], scalar2=None,
                                                                op0=ALU.add), reads=[TMPD, esink], writes=[RD])
            K.op("dve", lambda h, r0=r0: h.reciprocal(out=RD[r0:r0 + 64, 0:NS], in_=RD[r0:r0 + 64, 0:NS]),
                 reads=[RD], writes=[RD])
            K.op("dve", lambda h, r0=r0, j=j, vo=vo: h.tensor_tensor(out=TMP3[r0:r0 + 64, 0, 0:NS], in0=POWN[r0:r0 + 64, j, :],
                                                                      in1=VO[r0:r0 + 64, vo, :], op=ALU.mult),
                 reads=[POWN, VO], writes=[TMP3])
            K.op("dve", lambda h, r0=r0, cs=cs: h.tensor_tensor(out=TMP3[r0:r0 + 64, 0, 0:NS], in0=TMP3[r0:r0 + 64, 0, 0:NS],
                                                                 in1=PVS[r0:r0 + 64, cs], op=ALU.add),
                 reads=[TMP3, PVS], writes=[TMP3])
            K.op("dve", lambda h, r0=r0, j=j: h.tensor_tensor(out=YB[r0:r0 + 64, j, 0:NS], in0=TMP3[r0:r0 + 64, 0, 0:NS],
                                                               in1=RD[r0:r0 + 64, 0:NS], op=ALU.mult),
                 reads=[TMP3, RD], writes=[YB])

    def sample_attention_d(l, NUM, DEN):
        PVSp = PV[1]
        STSp = ST[1]
        for s in range(NS):
            for g in range(3):
                heads = [(2 * g + ph, g, ph, 0, ph) for ph in range(2)]
                sample_cache_heads(l, GROUPS[g][0], s, heads, STSp, PVSp)
        K.op("act", ev("act", PVS[:, 0:6 * NS], PVSp[:, 0:6 * NS]), reads=[PVSp], writes=[PVS])
        for g in range(3):
            own_key(l, [(0, 128, g, 2 + g)], 3 + g)
            for ph in range(2):
                r0, o0 = ph * 64, (1 - ph) * 64
                hd = 2 * g + ph
                cs = slice(hd * NS, (hd + 1) * NS)
                K.op("act", ev("act", DEN[r0:r0 + 64, g, 0:NS], PVS[o0:o0 + 64, cs]), reads=[PVS], writes=[DEN])
                K.op("act", ev("act", NUM[r0:r0 + 64, g, 0:NS], PVS[r0:r0 + 64, cs]), reads=[PVS], writes=[NUM])
            K.op("dve", lambda h, g=g: h.tensor_tensor(out=DEN[:, g, 0:NS], in0=DEN[:, g, 0:NS], in1=POWN[:, 3 + g, :],
                                                        op=ALU.add), reads=[DEN, POWN], writes=[DEN])
            K.op("dve", lambda h, g=g: h.tensor_tensor(out=TMPD[:, 0:NS], in0=POWN[:, 3 + g, :], in1=VO[:, 2 + g, :],
                                                        op=ALU.mult), reads=[POWN, VO], writes=[TMPD])
            K.op("dve", lambda h, g=g: h.tensor_tensor(out=NUM[:, g, 0:NS], in0=NUM[:, g, 0:NS], in1=TMPD[:, 0:NS],
                                                        op=ALU.add), reads=[NUM, TMPD], writes=[NUM])

    for l in range(NL):
        K.dma("sp", gbc[:], norm_g[l:l + 1, :].partition_broadcast(128), writes=[gbc], semb=gbc)
        K.dma("sp", lng[:, 0, :], vln_g[l:l + 1, :].partition_broadcast(128), writes=[lng], semb=lng)
        K.dma("sp", lng[:, 1, :], vln_b[l:l + 1, :].partition_broadcast(128), writes=[lng], semb=lng)
        for T in range(NT + 1):
            if DBG["tiles"] is None or (l, T) in DBG["tiles"]:
                tile(l, T)
    K.final_wait("sp")
    K.emit()
    es.close()
    return nc


_PROG = {}


def kernel(**inputs):
    f32 = lambda a: np.ascontiguousarray(np.asarray(a, dtype=np.float32))
    inp = {k: f32(v) for k, v in inputs.items()}
    if "nc" not in _PROG:
        _PROG["nc"] = build_program()
        _PROG["consts"] = _host_consts()
    nc = _PROG["nc"]
    consts = _PROG["consts"]
    shared = {k: inp[k] for k in ("norm_g", "w_in", "conv_w", "attn_sinks", "v_ln_g", "v_ln_b", "w_spatial",
                                  "b_spatial", "w_branch", "w_merge", "w_out")}
    shared["final_norm_g"] = inp["final_norm_g"].reshape(1, D)
    shared.update(consts)
    in_maps = []
    for c in range(NCORES):
        sl = slice(c * NS, (c + 1) * NS)
        m = dict(shared)
        m["x"] = inp["x_prompt"][c % 4]
        m["xs"] = inp["x_sample"][sl, 0, :]
        m["state_conv"] = inp["state_conv"][:, sl]
        m["cache_swa"] = inp["cache_swa_kv"][:, sl].reshape(NL, NS, 128, 256)
        m["cache_d0"] = inp["cache_dil1_kv"][:, sl].reshape(NL, NS, 128, 256)
        m["cache_d1"] = inp["cache_dil4_kv"][:, sl].reshape(NL, NS, 512, 256)
        m["cache_d2"] = inp["cache_dil16_kv"][:, sl].reshape(NL, NS, 2048, 256)
        in_maps.append({k: np.ascontiguousarray(v) for k, v in m.items()})
    res = run_bass_kernel_spmd(nc, in_maps, core_ids=list(range(NCORES))).results
    cat = lambda k, ax: np.concatenate([res[c][k] for c in range(NCORES)], axis=ax)
    stack4 = lambda k: np.stack([res[c][k] for c in range(4)], axis=0)
    y_prompt = stack4("y_p")
    y_sample = cat("y_s", 0).reshape(NCORES * NS, 1, D)
    conv_p = np.stack([res[c]["o_conv_p"] for c in range(4)], axis=1)
    conv_s = cat("o_conv_s", 1)
    outs = [y_prompt, y_sample, conv_p, conv_s]
    for nm, n in (("sw", 128), ("d0", 128), ("d1", 512), ("d2", 2048)):
        outs.append(np.stack([res[c][f"o_{nm}_p"] for c in range(4)], axis=1).reshape(NL, 4, n, 2, 2, 64))
        outs.append(cat(f"o_{nm}_s", 1).reshape(NL, NCORES * NS, n, 2, 2, 64))
    outs.append(cat("o_cv_s", 1).reshape(NL, NCORES * NS, 1, 384))
    return tuple(np.ascontiguousarray(o.astype(np.float32)) for o in outs)
```
